# Optimizing a Trainium2 kernel written in Bass

```python
import jax, jax.numpy as jnp
from jax import lax
import numpy as np

D_MODEL = 1024
BATCH = 8
SEQ = 4096
DEPTH = 2

D_MIX = D_MODEL
LRU_WIDTH = D_MIX // 2
GMLP_WIDTH = D_MIX - LRU_WIDTH
LRU_HEADS = 8
LRU_HEAD_DIM = LRU_WIDTH // LRU_HEADS
GMLP_HEADS = 8
GMLP_HEAD_DIM = GMLP_WIDTH // GMLP_HEADS
CONV_WIDTH = 4
RG_LRU_C = 8.0
CHUNK = 128
D_FF = ((8 * D_MODEL // 3 + 127) // 128) * 128
N_MOD = 9
IN_COLS = 2 * LRU_WIDTH + 2 * GMLP_WIDTH
EPS = 1e-6

kernel_name = "hybrid_rglru_gmlp_macaron_adaln"


def rms_norm(x, g):
    x32 = x.astype(jnp.float32)
    y = x32 * lax.rsqrt(jnp.mean(x32 * x32, axis=-1, keepdims=True) + EPS)
    return (y * g.astype(jnp.float32)).astype(x.dtype)


def layer_norm(x, g):
    x32 = x.astype(jnp.float32)
    mu = jnp.mean(x32, axis=-1, keepdims=True)
    var = jnp.mean(jnp.square(x32 - mu), axis=-1, keepdims=True)
    return ((x32 - mu) * lax.rsqrt(var + EPS) * g.astype(jnp.float32)).astype(x.dtype)


def modulate(h, shift, scale):
    return h * (1.0 + scale) + shift


def swiglu(h, w_gu, w_down):
    g, u = jnp.split(h @ w_gu, 2, axis=-1)
    return (jax.nn.silu(g) * u) @ w_down


def causal_depthwise_conv(x, w, b):
    S = x.shape[1]
    xp = jnp.pad(x, ((0, 0), (CONV_WIDTH - 1, 0), (0, 0)))
    y = b
    for k in range(CONV_WIDTH):
        y = y + xp[:, k:k + S] * w[k]
    return y


def _lin_rec_combine(left, right):
    a1, b1 = left
    a2, b2 = right
    return a1 * a2, a2 * b1 + b2


def rg_lru(xb, wa, ba, wx, bx, lam):
    B, S, _ = xb.shape
    xh = xb.reshape(B, S, LRU_HEADS, LRU_HEAD_DIM)
    r = jax.nn.sigmoid(jnp.einsum('bshd,hde->bshe', xh, wa) + ba).reshape(B, S, LRU_WIDTH)
    i = jax.nn.sigmoid(jnp.einsum('bshd,hde->bshe', xh, wx) + bx).reshape(B, S, LRU_WIDTH)
    log_a = RG_LRU_C * r.astype(jnp.float32) * jax.nn.log_sigmoid(lam.astype(jnp.float32))
    a = jnp.exp(log_a)
    mult = jnp.sqrt(-jnp.expm1(2.0 * log_a))
    inp = mult * (i * xb).astype(jnp.float32)
    _, h = lax.associative_scan(_lin_rec_combine, (a, inp), axis=1)
    return h.astype(xb.dtype)


def chunked_gmlp(u, v, v_norm, spatial_w, spatial_b):
    B, S, _ = u.shape
    nc = S // CHUNK
    u = jax.nn.gelu(u)
    v = jax.nn.gelu(v)
    vh = v.reshape(B, nc, CHUNK, GMLP_HEADS, GMLP_HEAD_DIM)
    vh = layer_norm(vh, v_norm.reshape(GMLP_HEADS, GMLP_HEAD_DIM))
    mask = jnp.tril(jnp.ones((CHUNK, CHUNK), dtype=spatial_w.dtype))
    ws = spatial_w * mask
    z = jnp.einsum('hts,bnshd->bnthd', ws, vh) + spatial_b.T[:, :, None]
    return u * z.reshape(B, S, GMLP_WIDTH)


def setup_inputs(seed: int = 0) -> dict:
    key = jax.random.key(seed)
    ks = jax.random.split(key, 32)
    f32 = jnp.float32
    L, D = DEPTH, D_MODEL

    def nrm(k, shape, scale):
        return jax.random.normal(k, shape, f32) * scale

    def gain(k, shape):
        return 1.0 + 0.05 * jax.random.normal(k, shape, f32)

    a0 = jax.random.uniform(ks[12], (L, LRU_WIDTH), f32, minval=0.9, maxval=0.999)
    lru_lambda = jnp.log(a0) - jnp.log1p(-a0)
    return {
        'x': jax.random.normal(ks[0], (BATCH, SEQ, D), f32),
        'c': jax.random.normal(ks[1], (BATCH, D), f32),
        'w_ada': nrm(ks[2], (L, D, N_MOD * D), 0.5 * D ** -0.5),
        'b_ada': nrm(ks[3], (L, N_MOD * D), 0.02),
        'ffn1_norm': gain(ks[4], (L, D)),
        'ffn1_w_gu': nrm(ks[5], (L, D, 2 * D_FF), D ** -0.5),
        'ffn1_w_down': nrm(ks[6], (L, D_FF, D), D_FF ** -0.5),
        'mix_norm': gain(ks[7], (L, D)),
        'w_in': nrm(ks[8], (L, D, IN_COLS), D ** -0.5),
        'conv_w': nrm(ks[9], (L, CONV_WIDTH, LRU_WIDTH), CONV_WIDTH ** -0.5),
        'conv_b': nrm(ks[10], (L, LRU_WIDTH), 0.02),
        'gate_a_w': nrm(ks[11], (L, LRU_HEADS, LRU_HEAD_DIM, LRU_HEAD_DIM), LRU_HEAD_DIM ** -0.5),
        'gate_a_b': nrm(ks[13], (L, LRU_HEADS, LRU_HEAD_DIM), 0.02),
        'gate_x_w': nrm(ks[14], (L, LRU_HEADS, LRU_HEAD_DIM, LRU_HEAD_DIM), LRU_HEAD_DIM ** -0.5),
        'gate_x_b': nrm(ks[15], (L, LRU_HEADS, LRU_HEAD_DIM), 0.02),
        'lru_lambda': lru_lambda,
        'v_norm': gain(ks[16], (L, GMLP_WIDTH)),
        'spatial_w': nrm(ks[17], (L, GMLP_HEADS, CHUNK, CHUNK), CHUNK ** -0.5),
        'spatial_b': nrm(ks[18], (L, GMLP_HEADS, CHUNK), 0.02),
        'lru_out_norm': gain(ks[19], (L, LRU_WIDTH)),
        'gmlp_out_norm': gain(ks[20], (L, GMLP_WIDTH)),
        'w_out': nrm(ks[21], (L, D_MIX, D), D_MIX ** -0.5),
        'ffn2_norm': gain(ks[22], (L, D)),
        'ffn2_w_gu': nrm(ks[23], (L, D, 2 * D_FF), D ** -0.5),
        'ffn2_w_down': nrm(ks[24], (L, D_FF, D), D_FF ** -0.5),
        'final_norm': gain(ks[25], (D,)),
    }


def reference(x, c, w_ada, b_ada, ffn1_norm, ffn1_w_gu, ffn1_w_down, mix_norm, w_in,
              conv_w, conv_b, gate_a_w, gate_a_b, gate_x_w, gate_x_b, lru_lambda,
              v_norm, spatial_w, spatial_b, lru_out_norm, gmlp_out_norm, w_out,
              ffn2_norm, ffn2_w_gu, ffn2_w_down, final_norm):
    B = x.shape[0]
    sc = jax.nn.silu(c)
    for l in range(DEPTH):
        mod = (sc @ w_ada[l] + b_ada[l]).reshape(B, N_MOD, 1, D_MODEL)

        h = modulate(rms_norm(x, ffn1_norm[l]), mod[:, 0], mod[:, 1])
        x = x + 0.5 * mod[:, 2] * swiglu(h, ffn1_w_gu[l], ffn1_w_down[l])

        h = modulate(rms_norm(x, mix_norm[l]), mod[:, 3], mod[:, 4])
        proj = h @ w_in[l]
        x_lru, g_lru, u, v = jnp.split(
            proj, [LRU_WIDTH, 2 * LRU_WIDTH, 2 * LRU_WIDTH + GMLP_WIDTH], axis=-1)
        x_lru = causal_depthwise_conv(x_lru, conv_w[l], conv_b[l])
        y_lru = rg_lru(x_lru, gate_a_w[l], gate_a_b[l], gate_x_w[l], gate_x_b[l],
                       lru_lambda[l]) * jax.nn.gelu(g_lru)
        y_gmlp = chunked_gmlp(u, v, v_norm[l], spatial_w[l], spatial_b[l])
        y = jnp.concatenate([rms_norm(y_lru, lru_out_norm[l]),
                             rms_norm(y_gmlp, gmlp_out_norm[l])], axis=-1)
        x = x + mod[:, 5] * (y @ w_out[l])

        h = modulate(rms_norm(x, ffn2_norm[l]), mod[:, 6], mod[:, 7])
        x = x + 0.5 * mod[:, 8] * swiglu(h, ffn2_w_gu[l], ffn2_w_down[l])
    return rms_norm(x, final_norm)
```

```python
import numpy as np
from contextlib import ExitStack
import concourse.bass as bass
import concourse.mybir as mybir
from concourse.bass_utils import run_bass_kernel_spmd

F32 = mybir.dt.float32
F32R = mybir.dt.float32r
BF16 = mybir.dt.bfloat16
AF = mybir.ActivationFunctionType
ALU = mybir.AluOpType

D = 1024
DFF = 2816
NF = DFF // 128
NSPC = 144
EPS = 1e-6
TT = 1024
ST = 512
NS = TT // ST
NSLOT = 6
SCHEDULE = True
SLOT = 4096

ENGS = ("pe", "act", "dve", "pool", "sp")


class Op:
    __slots__ = ("eng", "fn", "deps", "signal", "sem", "val", "dma_sem", "dma_need",
                 "idx", "dur", "aset", "sdeps", "succs", "prio", "npred", "fin", "nbytes", "region", "start")

    def __init__(self, eng, fn, dma_sem):
        self.eng = eng
        self.fn = fn
        self.deps = set()
        self.dma_need = {}
        self.signal = False
        self.sem = None
        self.val = 0
        self.dma_sem = dma_sem
        self.dur = 0.2
        self.aset = None
        self.nbytes = 0


class Prog:
    def __init__(self, same_engine_sync=("act", "dve", "pool")):
        self.ops = {e: [] for e in ENGS}
        self.last_writer = {}
        self.readers = {}
        self.same_engine_sync = set(same_engine_sync)
        self.dma_counts = {}
        self.all = []
        self.region = None

    def add(self, eng, fn, reads=(), writes=(), dma_sem=None, dur=0.2, aset=None, nbytes=0):
        op = Op(eng, fn, dma_sem)
        op.idx = len(self.all)
        op.region = self.region
        op.dur = dur
        op.aset = aset
        op.nbytes = nbytes
        self.all.append(op)
        deps = op.deps
        for k in reads:
            w = self.last_writer.get(k)
            if w is not None:
                deps.add(w)
        for k in writes:
            w = self.last_writer.get(k)
            if w is not None:
                deps.add(w)
            r = self.readers.get(k)
            if r:
                deps.update(r)
        op.sdeps = list(deps)
        for d in list(deps):
            if d.dma_sem is not None:
                deps.discard(d)
                op.dma_need[d.dma_sem] = self.dma_counts[d.dma_sem]
            elif d.eng == eng and eng not in self.same_engine_sync:
                deps.discard(d)
        for k in reads:
            self.readers.setdefault(k, []).append(op)
        for k in writes:
            self.last_writer[k] = op
            self.readers[k] = []
        if dma_sem is not None:
            c = self.dma_counts.get(dma_sem, 0) + 16
            self.dma_counts[dma_sem] = c
            op.val = c
        self.ops[eng].append(op)
        return op

    def schedule(self, reorder=("pe", "act", "dve"), lat=0.12, topk=12, swpen=1.3):
        import heapq
        ops = self.all
        for op in ops:
            op.succs = []
        for op in ops:
            op.npred = len(op.sdeps)
            for d in op.sdeps:
                d.succs.append(op)
        for op in reversed(ops):
            m = 0.0
            for su in op.succs:
                if su.prio > m:
                    m = su.prio
            op.prio = op.dur + m
        ready = {e: [] for e in ENGS}
        inorder = {e: (e not in reorder) for e in ENGS}
        nextidx = {e: 0 for e in ENGS}
        orig = {e: list(self.ops[e]) for e in ENGS}
        rdy = {}
        for op in ops:
            if op.npred == 0:
                rdy[op] = 0.0
                heapq.heappush(ready[op.eng], ((op.idx if inorder[op.eng] else -op.prio), op.idx, op))
        free = {e: 0.0 for e in ENGS}
        dma_free = 0.0
        cur_set = None
        new = {e: [] for e in ENGS}
        done = 0
        n = len(ops)
        sched = set()
        while done < n:
            best = None
            popped = {}
            for e in ENGS:
                h = ready[e]
                if inorder[e]:
                    if nextidx[e] >= len(orig[e]):
                        continue
                    tgt = orig[e][nextidx[e]]
                    if tgt.npred > 0 or tgt not in rdy:
                        continue
                    cands = [tgt]
                else:
                    cands = []
                    pl = []
                    while h and len(cands) < topk:
                        x = heapq.heappop(h)
                        if x[2] in sched:
                            continue
                        pl.append(x)
                        cands.append(x[2])
                    popped[e] = pl
                for op in cands:
                    est = max(free[e], rdy[op])
                    kest = est
                    if e == "act" and op.aset is not None and op.aset != cur_set:
                        est += 1.3
                        kest += swpen
                    key = (kest, -op.prio, est)
                    if best is None or key < best[0]:
                        best = (key, op)
            assert best is not None, "scheduler deadlock"
            (_, _, est), op = best
            e = op.eng
            sched.add(op)
            for e2, pl in popped.items():
                for x in pl:
                    if x[2] is not op:
                        heapq.heappush(ready[e2], x)
            if inorder[e]:
                nextidx[e] += 1
            if e == "act" and op.aset is not None:
                cur_set = op.aset
            op.start = est
            if op.dma_sem is not None:
                free[e] = est + 0.06
                st_ = max(est, dma_free)
                dma_free = st_ + op.nbytes / 330e3
                op.fin = dma_free + 2.0
            else:
                free[e] = est + op.dur
                op.fin = free[e]
            new[e].append(op)
            done += 1
            for su in op.succs:
                su.npred -= 1
                t = op.fin + (lat if su.eng != e else 0.0)
                if rdy.get(su, 0.0) < t:
                    rdy[su] = t
                if su.npred == 0:
                    heapq.heappush(ready[su.eng], ((su.idx if inorder[su.eng] else -su.prio), su.idx, su))
        self.ops = new
        return max(op.fin for op in ops)

    def emit(self, nc, stack, final_waits=()):
        for e in ENGS:
            for op in self.ops[e]:
                for d in op.deps:
                    d.signal = True
        sems = {}
        for e in ENGS:
            sems[("eng", e)] = stack.enter_context(nc.semaphore("s_" + e))
        for k in self.dma_counts:
            sems[("dma", k)] = stack.enter_context(nc.semaphore("d_" + str(k)))
        for e in ENGS:
            c = 0
            for op in self.ops[e]:
                if op.dma_sem is not None:
                    op.sem = ("dma", op.dma_sem)
                    continue
                op.sem = ("eng", e)
                if op.signal:
                    c += 1
                    op.val = c
        block = stack.enter_context(nc.Block())
        engmap = {"pe": "tensor", "act": "scalar", "dve": "vector", "pool": "gpsimd", "sp": "sync"}
        stats = {}
        for e in ENGS:
            ops = self.ops[e]
            if not ops:
                continue

            def body(eng, ops=ops, e=e):
                waited = {}
                nw = 0
                for op in ops:
                    need = {}
                    for d in op.deps:
                        if d.val > need.get(d.sem, 0):
                            need[d.sem] = d.val
                    for k, v in op.dma_need.items():
                        need[("dma", k)] = v
                    for s, v in need.items():
                        if waited.get(s, 0) < v:
                            eng.wait_ge(sems[s], v)
                            waited[s] = v
                            nw += 1
                    ins = op.fn(eng)
                    if op.dma_sem is not None:
                        ins.then_inc(sems[op.sem], 16)
                    elif op.signal:
                        ins.then_inc(sems[op.sem], 1)
                if e == "sp":
                    for (k, v) in final_waits:
                        eng.wait_ge(sems[("dma", k)], v)
                stats[e] = (len(ops), nw)

            getattr(block, engmap[e])(body)
        return stats


C_BADA = 0
C_N1 = 72
C_NM = 80
C_N2 = 88
C_CW = 96
C_CB = 112
C_BA = 116
C_BX = 120
C_LAM = 124
C_LN = 128
C_GN = 132
C_FN = 136
C_VN = 144
NSPC = 148


def _cols(v, n):
    return np.ascontiguousarray(v.reshape(n, 128).T)


def prep_inputs(inp, L):
    f = lambda a: np.asarray(a, dtype=np.float32)
    B = inp["x"].shape[0]
    shared = {}
    shared["wada"] = np.ascontiguousarray(
        f(inp["w_ada"])[:L].reshape(L, 8, 128, 18, 512).transpose(0, 3, 2, 1, 4)).reshape(L, 18, 128, SLOT)
    sp = np.zeros((L, 128, NSPC), np.float32)
    for l in range(L):
        sp[l, :, C_BADA:C_BADA + 72] = _cols(f(inp["b_ada"])[l], 72)
        sp[l, :, C_N1:C_N1 + 8] = _cols(f(inp["ffn1_norm"])[l], 8)
        sp[l, :, C_NM:C_NM + 8] = _cols(f(inp["mix_norm"])[l], 8)
        sp[l, :, C_N2:C_N2 + 8] = _cols(f(inp["ffn2_norm"])[l], 8)
        for k in range(4):
            sp[l, :, C_CW + 4 * k:C_CW + 4 * k + 4] = _cols(f(inp["conv_w"])[l, k], 4)
        sp[l, :, C_CB:C_CB + 4] = _cols(f(inp["conv_b"])[l], 4)
        sp[l, :, C_BA:C_BA + 4] = _cols(f(inp["gate_a_b"])[l].reshape(-1), 4)
        sp[l, :, C_BX:C_BX + 4] = _cols(f(inp["gate_x_b"])[l].reshape(-1), 4)
        sp[l, :, C_LAM:C_LAM + 4] = _cols(f(inp["lru_lambda"])[l], 4)
        sp[l, :, C_LN:C_LN + 4] = _cols(f(inp["lru_out_norm"])[l], 4)
        sp[l, :, C_GN:C_GN + 4] = _cols(f(inp["gmlp_out_norm"])[l], 4)
        sp[l, :, C_FN:C_FN + 8] = _cols(f(inp["final_norm"]), 8)
        sp[l, :, C_VN:C_VN + 4] = _cols(f(inp["v_norm"])[l], 4)
    shared["sp"] = sp
    gates = np.zeros((L, 128, 2, 4, 128), np.float32)
    for l in range(L):
        for gi, nm in enumerate(("gate_a_w", "gate_x_w")):
            w = f(inp[nm])[l]
            for cc in range(4):
                gates[l, 0:64, gi, cc, 0:64] = w[2 * cc]
                gates[l, 64:128, gi, cc, 64:128] = w[2 * cc + 1]
    shared["gates"] = gates
    shared["wst"] = np.ascontiguousarray(f(inp["spatial_w"])[:L].transpose(0, 3, 1, 2))
    shared["maskt"] = np.triu(np.ones((128, 128), np.float32))
    shared["ident"] = np.eye(128, dtype=np.float32)
    sbb = f(inp["spatial_b"])[:L]
    shared["sbb"] = np.ascontiguousarray(np.repeat(sbb, 64, axis=1).reshape(L, 4, 128, 128).transpose(0, 2, 1, 3))
    wgu = np.stack([f(inp["ffn1_w_gu"])[:L], f(inp["ffn2_w_gu"])[:L]], axis=1)
    shared["wgu"] = np.ascontiguousarray(
        wgu.reshape(L, 2, 8, 128, 2, 11, 2, 128).transpose(0, 1, 5, 3, 6, 2, 4, 7)).reshape(L, 2, 11, 128, SLOT)
    wd = np.stack([f(inp["ffn1_w_down"])[:L], f(inp["ffn2_w_down"])[:L]], axis=1)
    shared["wd"] = np.ascontiguousarray(
        wd.reshape(L, 2, NF, 128, 8, 128).transpose(0, 1, 4, 3, 2, 5)).reshape(L, 2, 8, 128, NF * 128)
    win = f(inp["w_in"])[:L]
    p012 = win[:, :, :1536].reshape(L, 8, 128, 3, 4, 128).transpose(0, 3, 2, 4, 1, 5).reshape(L, 3, 128, SLOT)
    p3 = win[:, :, 1536:].reshape(L, 8, 128, 512).transpose(0, 2, 1, 3).reshape(L, 1, 128, SLOT)
    shared["win"] = np.ascontiguousarray(np.concatenate([p012, p3], axis=1))
    wout = f(inp["w_out"])[:L]
    shared["wout"] = np.ascontiguousarray(
        wout.reshape(L, 8, 128, 2, 4, 128).transpose(0, 3, 2, 4, 1, 5)).reshape(L, 2, 128, SLOT)
    per_core = []
    x = f(inp["x"])
    c = f(inp["c"])
    for b in range(B):
        d = dict(shared)
        d["xT"] = np.ascontiguousarray(x[b].T)
        d["cv"] = _cols(c[b], 8)
        per_core.append(d)
    return per_core


def build_program(S, L, do_final=True):
    NT = S // TT
    nc = bass.Bass("TRN2", target_bir_lowering=False)
    dr = lambda name, shape: nc.dram_tensor(name, shape, F32, kind="ExternalInput").ap()
    xT = dr("xT", [D, S])
    cv = dr("cv", [128, 8])
    wada = dr("wada", [L, 18, 128, SLOT])
    spd = dr("sp", [L, 128, NSPC])
    gates_d = dr("gates", [L, 128, 2, 4, 128])
    wst_d = dr("wst", [L, 128, 8, 128])
    maskt_d = dr("maskt", [128, 128])
    ident_d = dr("ident", [128, 128])
    sbb_d = dr("sbb", [L, 128, 4, 128])
    wgu_d = dr("wgu", [L, 2, 11, 128, SLOT])
    wd_d = dr("wd", [L, 2, 8, 128, NF * 128])
    win_d = dr("win", [L, 4, 128, SLOT])
    wout_d = dr("wout", [L, 2, 128, SLOT])
    outT = nc.dram_tensor("outT", [D, S], F32, kind="ExternalOutput").ap()
    AX = mybir.AxisListType.X

    P = Prog()
    with ExitStack() as st:
        sb = lambda name, shape, dt: st.enter_context(nc.sbuf_tensor("sb_" + name, shape, dt))
        xh = sb("xh", [128, 3 * 8 * ST], F32)
        hbuf = sb("hbuf", [128, 8 * TT], BF16)
        abuf = sb("abuf", [128, NF * TT], BF16)
        ring = sb("ring", [128, NSLOT * SLOT], BF16)
        ybuf = sb("ybuf", [128, 8 * ST], BF16)
        sqb = sb("sqb", [128, 4 * ST], BF16)
        sgb = sb("sgb", [128, 3 * ST], F32)
        rsb = sb("rsb", [128, 2 * ST], F32)
        spt = sb("spt", [128, L * NSPC], F32)
        modt = sb("modt", [128, L * 72], F32)
        dvt = sb("dvt", [128, L * 64], F32)
        gat = sb("gat", [128, L * 1024], BF16)
        wsm = sb("wsm", [128, L * 1024], BF16)
        sbbt = sb("sbbt", [128, L * 512], F32)
        ybl2 = sb("ybl2", [128, 4 * ST], BF16)
        stg = ybl2[:].bitcast(F32)
        maskt = sb("maskt", [128, 128], F32)
        ident = sb("ident", [128, 128], F32)
        ones = sb("ones", [128, 128], BF16)
        cvt = sb("cvt", [128, 8], F32)
        scb = sb("scb", [128, 8 * 128], BF16)
        hist = sb("hist", [128, L * 4 * 4], F32)
        hstate = sb("hstate", [128, L * 4], F32)
        lns = sb("lns", [128, 5 * 32], F32)
        lt = sb("lt", [128, 8], F32)
        psb = [st.enter_context(nc.psum_tensor("ps%d" % i, [128, 512], F32)) for i in range(8)]

        cur = {"t": 0}

        def xbuf(t, s):
            return (2 * t + s) % 3

        def X(c, s):
            b_ = xbuf(cur["t"], s)
            return xh[:, (b_ * 8 + c) * ST:(b_ * 8 + c + 1) * ST]

        def xk(c, s):
            return ("x", xbuf(cur["t"], s), c)

        def H(c, s):
            return hbuf[:, c * TT + s * ST: c * TT + (s + 1) * ST]

        def A(f_, s):
            return abuf[:, f_ * TT + s * ST: f_ * TT + (s + 1) * ST]

        def U(u):
            return abuf[:, u * TT:(u + 1) * TT].bitcast(F32)

        def ukeys(u):
            return [("a", u, 0), ("a", u, 1)]

        def slot_ap(sl):
            return ring[:, sl * SLOT:(sl + 1) * SLOT]

        def SQ(i):
            return sqb[:, i * ST:(i + 1) * ST]

        def SG(i):
            return sgb[:, i * ST:(i + 1) * ST]

        def RS(i):
            return rsb[:, i * ST:(i + 1) * ST]

        def SP(l, col, n=1):
            return spt[:, l * NSPC + col: l * NSPC + col + n]

        def MOD(l, m, c):
            return modt[:, l * 72 + m * 8 + c: l * 72 + m * 8 + c + 1]

        DV_GM1, DV_HG1, DV_GMM, DV_GM2, DV_HG2, DV_C8H, DV_C16H, DV_HB = 0, 8, 16, 24, 32, 40, 44, 48

        def DV(l, base, c, n=1):
            return dvt[:, l * 64 + base + c: l * 64 + base + c + n]

        cnt = {"bank": 0, "slot": 0, "sq": 0, "sg": 0, "rs": 0}

        POOLS = {
            "bank": {"all": [0, 1, 2, 3, 4, 5], "L": [0, 1, 2], "G": [3, 4, 5]},
            "sq": {"all": [0, 1, 2, 3], "L": [0, 1], "G": [2, 3]},
            "sg": {"all": [0, 1, 2], "L": [0, 1], "G": [2]},
            "rs": {"all": [0, 1], "L": [0], "G": [1]},
            "slot": {"all": list(range(NSLOT))},
        }
        pool = {"cur": "all"}
        pcnt = {}

        def nxt(name, n=None):
            p = pool["cur"] if pool["cur"] in POOLS[name] else "all"
            lst = POOLS[name][p]
            i = pcnt.get((name, p), 0)
            pcnt[(name, p)] = (i + 1) % len(lst)
            return lst[i]

        def next_bank():
            return nxt("bank")

        def fsz(ap):
            n = 1
            for d in ap.shape[1:]:
                n *= d
            return n

        ASET = {AF.Silu: "silu", AF.Gelu_apprx_tanh: "gelu", AF.Tanh: "gelu", AF.Exp: "lnexp", AF.Ln: "lnexp"}

        def mm(out, lhsT, rhs, start, stop, reads, writes):
            P.add("pe", lambda e: e.matmul(out, lhsT=lhsT, rhs=rhs, start=start, stop=stop), reads, writes,
                  dur=0.004 + fsz(rhs) / 2400.0)

        def act(out, in_, func, reads, writes, **kw):
            nap = sum(1 for v in kw.values() if not isinstance(v, (int, float)))
            P.add("act", lambda e: e.activation(out=out, in_=in_, func=func, **kw), reads, writes,
                  dur=0.2 + fsz(out) / 1150.0 + 0.07 * nap, aset=ASET.get(func))

        def tt(out, in0, in1, op, reads, writes):
            P.add("dve", lambda e: e.tensor_tensor(out=out, in0=in0, in1=in1, op=op), reads, writes, dur=0.1 + fsz(out) / 870.0)

        def ts(out, in0, s1, s2, op0, op1, reads, writes):
            P.add("dve", lambda e: e.tensor_scalar(out=out, in0=in0, scalar1=s1, scalar2=s2, op0=op0, op1=op1), reads, writes,
                  dur=0.1 + fsz(out) / 900.0)

        def stt(out, in0, scalar, in1, op0, op1, reads, writes):
            P.add("dve", lambda e: e.scalar_tensor_tensor(out=out, in0=in0, scalar=scalar, in1=in1, op0=op0, op1=op1), reads, writes,
                  dur=0.12 + fsz(out) / 820.0)

        def red(out, in_, reads, writes):
            P.add("dve", lambda e: e.tensor_reduce(out=out, in_=in_, axis=AX, op=ALU.add), reads, writes, dur=0.1 + fsz(in_) / 1000.0)

        def dma(eng, out, in_, reads, writes, sem):
            nb = 4
            for d in in_.shape:
                nb *= d
            P.add(eng, lambda e: e.dma_start(out=out, in_=in_), reads, writes, dma_sem=sem, nbytes=nb)

        def ring_load(src, extra_reads=()):
            sl = nxt("slot", NSLOT)
            n = src.shape[-1]
            dst = slot_ap(sl)
            dma("pool", dst[:, 0:n].rearrange("p (a b) -> p a b", a=2), src.rearrange("p (a b) -> p a b", a=2),
                list(extra_reads), [("ring", sl)], "ring%d" % sl)
            return sl, dst

        def rstd_from(ps_ap, pkeys, n_feat):
            r = nxt("rs")
            act(RS(r), ps_ap, AF.Ln, pkeys, [("rs", r)], scale=1.0 / n_feat, bias=EPS)
            act(RS(r), RS(r), AF.Exp, [("rs", r)], [("rs", r)], scale=-0.5)
            return r

        def load_x(t, s):
            b_ = xbuf(t, s)
            t0 = t * TT + s * ST
            dma("sp", xh[:, b_ * 8 * ST:(b_ + 1) * 8 * ST].rearrange("p (c t) -> p c t", c=8),
                xT.rearrange("(c p) t -> p c t", p=128)[:, :, t0:t0 + ST], [], [("x", b_, c) for c in range(8)], "xin%d" % b_)

        CONST = "const"
        for l in range(L):
            dma("sp", spt[:, l * NSPC:(l + 1) * NSPC], spd[l], [], [CONST], "c0")
        dma("sp", cvt[:], cv, [], [CONST], "c0")
        dma("sp", ident[:], ident_d, [], [CONST], "c0")
        load_x_early = True
        P.add("dve", lambda e: e.memset(ones[:], 1.0), [], ["ones"])
        P.add("dve", lambda e: e.memset(hist[:], 0.0), [], [("hist", l, cc) for l in range(L) for cc in range(4)])
        P.add("dve", lambda e: e.memset(hstate[:], 0.0), [], [("hstate", l, cc) for l in range(L) for cc in range(4)])
        for l in range(L):
            ts(DV(l, DV_HB, 0, 4), SP(l, C_BA, 4), 0.5, None, ALU.mult, ALU.bypass, [CONST], [("dvb", l)])
            ts(DV(l, DV_HB, 4, 4), SP(l, C_BX, 4), 0.5, None, ALU.mult, ALU.bypass, [CONST], [("dvb", l)])
        act(scb[:].rearrange("p (k m) -> p k m", k=8),
            cvt[:].rearrange("p (k o) -> p k o", o=1).to_broadcast([128, 8, 128]), AF.Silu, [CONST], ["scb"])

        def ada_piece(l, n, extra_reads=()):
            sl, w = ring_load(wada[l, n], extra_reads)
            b = next_bank()
            for kc in range(8):
                mm(psb[b][:], scb[:, kc * 128:(kc + 1) * 128], w[:, kc * 512:(kc + 1) * 512], kc == 0, kc == 7,
                   [("ring", sl), "scb"], [("ps", b)])
            g = nxt("sg")
            tt(SG(g).rearrange("p (q j) -> p q j", q=4), psb[b][:].rearrange("p (q j) -> p q j", q=4),
               ident[:].rearrange("p (o j) -> p o j", o=1).to_broadcast([128, 4, 128]), ALU.mult,
               [("ps", b), CONST], [("sg", g)])
            red(modt[:, l * 72 + 4 * n: l * 72 + 4 * n + 4], SG(g).rearrange("p (q j) -> p q j", q=4),
                [("sg", g)], [("modraw", l, n)])

        def ada_finish(l, g):
            c0 = l * 72 + g * 24
            pk = [("modraw", l, n) for n in range(6 * g, 6 * g + 6)]
            tt(modt[:, c0:c0 + 24], modt[:, c0:c0 + 24], SP(l, C_BADA + g * 24, 24), ALU.add, pk + [CONST], [("mod", l, g)] + pk)
            rk = [("mod", l, g), CONST]
            wk = [("dv", l, g)]
            ncol = (C_N1, C_NM, C_N2)[g]
            gmb = (DV_GM1, DV_GMM, DV_GM2)[g]
            stt(DV(l, gmb, 0, 8), modt[:, c0 + 8:c0 + 16], 1.0, SP(l, ncol, 8), ALU.add, ALU.mult, rk, wk)
            if g != 1:
                hgb = DV_HG1 if g == 0 else DV_HG2
                ts(DV(l, hgb, 0, 8), modt[:, c0 + 16:c0 + 24], 0.5, None, ALU.mult, ALU.bypass, rk, wk)

        def lam_consts(l):
            act(lt[:, 0:4], SP(l, C_LAM, 4), AF.Exp, [CONST], ["lt"], scale=-1.0)
            act(lt[:, 4:8], lt[:, 0:4], AF.Ln, ["lt"], ["lt"], bias=1.0)
            ts(DV(l, DV_C8H, 0, 4), lt[:, 4:8], -4.0, None, ALU.mult, ALU.bypass, ["lt"], [("dvl", l)])
            ts(DV(l, DV_C16H, 0, 4), lt[:, 4:8], -8.0, None, ALU.mult, ALU.bypass, ["lt"], [("dvl", l)])

        for l in range(L):
            lam_consts(l)
        load_x(0, 0)
        for n in range(6):
            ada_piece(0, n, extra_reads=["scb"] + [("x", xbuf(0, 0), c) for c in range(8)] if n == 0 else ())
        ada_finish(0, 0)
        dma("sp", maskt[:], maskt_d, [], ["maskt"], "c2")
        for l in range(L):
            dma("sp", sbbt[:, l * 512:(l + 1) * 512], sbb_d[l].rearrange("p c t -> p (c t)"), [], [("sbb", l)], "c2")
        stgk = ["stg"] + [("yl", 1, cc) for cc in range(4)]
        for l in range(L):
            dma("sp", stg, gates_d[l].rearrange("p g c j -> p (g c j)"), [], stgk, "c1")
            P.add("dve", lambda e, l=l: e.tensor_copy(out=gat[:, l * 1024:(l + 1) * 1024], in_=stg), stgk, [("gat", l)])
            dma("sp", stg, wst_d[l].rearrange("p h t -> p (h t)"), [], stgk, "c1")
            tt(wsm[:, l * 1024:(l + 1) * 1024].rearrange("p (h t) -> p h t", h=8),
               stg.rearrange("p (h t) -> p h t", h=8),
               maskt[:].rearrange("p (o t) -> p o t", o=1).to_broadcast([128, 8, 128]), ALU.mult, stgk + ["maskt"], [("wsm", l)])
        pending_ada = [(0, n) for n in range(6, 18)] + [(l, n) for l in range(1, L) for n in range(18)]

        def ada_some(k=1):
            for _ in range(k):
                if pending_ada:
                    l_, n_ = pending_ada.pop(0)
                    ada_piece(l_, n_)
                    if n_ % 6 == 5:
                        ada_finish(l_, n_ // 6)

        def ada_need(l, g):
            while pending_ada and (pending_ada[0][0], pending_ada[0][1] // 6) <= (l, g):
                ada_some(1)

        def sumsq(srcs, skeys, bank):
            n = len(srcs)
            for i, (src, sk) in enumerate(zip(srcs, skeys)):
                q = nxt("sq")
                act(SQ(q), src, AF.Square, sk, [("sq", q)])
                mm(psb[bank][:], ones[:], SQ(q), i == 0, i == n - 1, [("sq", q), "ones"], [("ps", bank)])

        def norm_to_h(l, s, gmbase, shift_m):
            nb = next_bank()
            sumsq([X(c, s) for c in range(8)], [[xk(c, s)] for c in range(8)], nb)
            r = rstd_from(psb[nb][:], [("ps", nb)], D)
            for c in range(8):
                g = nxt("sg")
                tt(SG(g), X(c, s), RS(r), ALU.mult, [xk(c, s), ("rs", r)], [("sg", g)])
                gq = shift_m // 3
                act(H(c, s), SG(g), AF.Identity, [("sg", g), ("dv", l, gq), ("mod", l, gq)], [("h", c, s)],
                    scale=DV(l, gmbase, c), bias=MOD(l, shift_m, c))

        def ffn(l, w, interleave_ada=False):
            P.region = ("ffn", cur["t"], l, w)
            ada_need(l, 0 if w == 0 else 2)
            gmbase = DV_GM1 if w == 0 else DV_GM2
            hgbase = DV_HG1 if w == 0 else DV_HG2
            shift_m = 0 if w == 0 else 6
            for s in range(NS):
                norm_to_h(l, s, gmbase, shift_m)
            for piece in range(11):
                sl, wsl = ring_load(wgu_d[l, w, piece])
                for fi in range(2):
                    f_ = 2 * piece + fi
                    for s in range(NS):
                        bg = next_bank()
                        bu = next_bank()
                        for gu, bb in ((0, bg), (1, bu)):
                            for kc in range(8):
                                o = ((fi * 8 + kc) * 2 + gu) * 128
                                mm(psb[bb][:], wsl[:, o:o + 128], H(kc, s), kc == 0, kc == 7,
                                   [("ring", sl), ("h", kc, s)], [("ps", bb)])
                        g = nxt("sg")
                        act(SG(g), psb[bg][:], AF.Silu, [("ps", bg)], [("sg", g)])
                        tt(A(f_, s), psb[bu][:], SG(g), ALU.mult, [("ps", bu), ("sg", g)], [("a", f_, s)])
                if interleave_ada and piece % 2 == 1:
                    ada_some(1)
            for o in range(8):
                sl, wsl = ring_load(wd_d[l, w, o])
                for s in range(NS):
                    b = next_bank()
                    for f_ in range(NF):
                        mm(psb[b][:], wsl[:, f_ * 128:(f_ + 1) * 128], A(f_, s), f_ == 0, f_ == NF - 1,
                           [("ring", sl), ("a", f_, s)], [("ps", b)])
                    stt(X(o, s), psb[b][:], DV(l, hgbase, o), X(o, s), ALU.mult, ALU.add,
                        [("ps", b), xk(o, s), ("dv", l, 0 if w == 0 else 2)], [xk(o, s)])
                if interleave_ada and o % 2 == 1:
                    ada_some(1)

        U_UU, U_VG, U_SQV, U_VH = 11, 15, 19, 20

        def mixer(l):
            P.region = ("mixer", cur["t"], l)
            ada_need(l, 1)
            W = {}
            for nm, src in (("xl", win_d[l, 0]), ("gl", win_d[l, 1]), ("u", win_d[l, 2]), ("v", win_d[l, 3]),
                            ("o0", wout_d[l, 0]), ("o1", wout_d[l, 1])):
                W[nm] = ring_load(src)
            for s in range(NS):
                pool["cur"] = "L"
                norm_to_h(l, s, DV_GMM, 3)
                lru(l, s, W)
            if cur["t"] == 0:
                ada_some(2)
            for s in range(NS):
                pool["cur"] = "G"
                gmlp(l, s, W)
                if s == NS - 1 and cur["t"] == 0:
                    ada_some(2)
                outproj(l, s, W)
            pool["cur"] = "all"

        def YL(par, cc):
            base = ybuf if par == 0 else ybl2
            return base[:, cc * ST:(cc + 1) * ST]

        def lru(l, s, W):
            sl_xl, w_xl = W["xl"]
            sl_gl, w_gl = W["gl"]
            dk = [("dvl", l), CONST]
            par = s % 2
            for cc in range(4):
                k = cc % 2
                XC, R, I_, MU, GG = U(0 + k), U(2 + k), U(4 + k), U(6 + k), U(8 + k)
                kx, kr, ki, km, kg = ukeys(0 + k), ukeys(2 + k), ukeys(4 + k), ukeys(6 + k), ukeys(8 + k)
                XB = abuf[:, 10 * TT + k * ST: 10 * TT + (k + 1) * ST]
                kxb = [("a", 10, k)]
                b = next_bank()
                for kc in range(8):
                    o = (cc * 8 + kc) * 128
                    mm(psb[b][:], w_xl[:, o:o + 128], H(kc, s), kc == 0, kc == 7, [("ring", sl_xl), ("h", kc, s)], [("ps", b)])
                hv = hist[:, (l * 4 + cc) * 4:(l * 4 + cc) * 4 + 3]
                khist = [("hist", l, cc)]
                act(XC, psb[b][:], AF.Identity, [("ps", b), CONST], kx, scale=SP(l, C_CW + 12 + cc), bias=SP(l, C_CB + cc))
                for kk in range(3):
                    d = 3 - kk
                    stt(XC[:, d:ST], psb[b][:, 0:ST - d], SP(l, C_CW + 4 * kk + cc), XC[:, d:ST], ALU.mult, ALU.add,
                        [("ps", b), CONST] + kx, kx)
                    stt(XC[:, 0:d], hv[:, kk:kk + d], SP(l, C_CW + 4 * kk + cc), XC[:, 0:d], ALU.mult, ALU.add,
                        khist + [CONST] + kx, kx)
                act(hv, psb[b][:, ST - 3:ST], AF.Copy, [("ps", b)], khist)
                act(XB, XC, AF.Copy, kx, kxb)
                for gi, dst, kd in ((0, R, kr), (1, I_, ki)):
                    bgt = next_bank()
                    gw = gat[:, l * 1024 + gi * 512 + cc * 128: l * 1024 + gi * 512 + (cc + 1) * 128]
                    mm(psb[bgt][:], gw, XB, True, True, kxb + [("gat", l)], [("ps", bgt)])
                    act(dst, psb[bgt][:], AF.Tanh, [("ps", bgt), ("dvb", l)], kd, scale=0.5, bias=DV(l, DV_HB + gi * 4, cc))
                b = next_bank()
                for kc in range(8):
                    o = (cc * 8 + kc) * 128
                    mm(psb[b][:], w_gl[:, o:o + 128], H(kc, s), kc == 0, kc == 7, [("ring", sl_gl), ("h", kc, s)], [("ps", b)])
                act(GG, psb[b][:], AF.Gelu_apprx_tanh, [("ps", b)], kg)
                act(MU, R, AF.Exp, kr + dk, km, scale=DV(l, DV_C16H, cc), bias=DV(l, DV_C16H, cc))
                act(MU, MU, AF.Ln, km, km, scale=-1.0, bias=1.0)
                act(MU, MU, AF.Exp, km, km, scale=0.5)
                act(R, R, AF.Exp, kr + dk, kr, scale=DV(l, DV_C8H, cc), bias=DV(l, DV_C8H, cc))
                stt(XC, I_, 1.0, XC, ALU.add, ALU.mult, ki + kx, kx)
                stt(MU, XC, 0.5, MU, ALU.mult, ALU.mult, kx + km, km)
                hs = hstate[:, l * 4 + cc:l * 4 + cc + 1]
                P.add("dve", lambda e, I_=I_, R=R, MU=MU, hs=hs: e.tensor_tensor_scan(
                    out=I_, data0=R, data1=MU, initial=hs, op0=ALU.mult, op1=ALU.add),
                    kr + km + [("hstate", l, cc)], ki, dur=1.35)
                act(hs, I_[:, ST - 1:ST], AF.Copy, ki, [("hstate", l, cc)])
                tt(GG, I_, GG, ALU.mult, ki + kg, kg)
                act(YL(par, cc), GG, AF.Identity, kg + [CONST], [("yl", par, cc)], scale=SP(l, C_LN + cc))
                q = nxt("sq")
                act(SQ(q), GG, AF.Square, kg, [("sq", q)])
                mm(psb[7][:], ones[:], SQ(q), cc == 0, cc == 3, [("sq", q), "ones"], [("ps", 7)])
            r = rstd_from(psb[7][:], [("ps", 7)], 512)
            for cc in range(4):
                tt(YL(par, cc), YL(par, cc), RS(r), ALU.mult, [("yl", par, cc), ("rs", r)], [("yl", par, cc)])

        def gmlp(l, s, W):
            sl_u, w_u = W["u"]
            sl_v, w_v = W["v"]
            for cc in range(4):
                b = next_bank()
                for kc in range(8):
                    o = (cc * 8 + kc) * 128
                    mm(psb[b][:], w_u[:, o:o + 128], H(kc, s), kc == 0, kc == 7, [("ring", sl_u), ("h", kc, s)], [("ps", b)])
                act(U(U_UU + cc), psb[b][:], AF.Gelu_apprx_tanh, [("ps", b)], ukeys(U_UU + cc))
            S1, S2, MEAN, VAR, RSTD = (lns[:, i * 32:(i + 1) * 32] for i in range(5))
            for j in range(4):
                b = next_bank()
                for kc in range(8):
                    mm(psb[b][:], H(kc, s)[:, j * 128:(j + 1) * 128], w_v[:, kc * 512:(kc + 1) * 512], kc == 0, kc == 7,
                       [("ring", sl_v), ("h", kc, s)], [("ps", b)])
                VG = U(U_VG + j)
                kv = ukeys(U_VG + j)
                act(VG, psb[b][:], AF.Gelu_apprx_tanh, [("ps", b)], kv)
                SQV = abuf[:, U_SQV * TT + (j % 2) * ST: U_SQV * TT + (j % 2 + 1) * ST]
                ksq = [("a", U_SQV, j % 2)]
                act(SQV, VG, AF.Square, kv, ksq)
                red(S1[:, j * 8:(j + 1) * 8], VG.rearrange("p (h d) -> p h d", h=8), kv, [("lns", 0, j)])
                red(S2[:, j * 8:(j + 1) * 8], SQV.rearrange("p (h d) -> p h d", h=8), ksq, [("lns", 1, j)])
            k01 = [("lns", 0, j) for j in range(4)] + [("lns", 1, j) for j in range(4)]
            ts(MEAN, S1, 1.0 / 64, None, ALU.mult, ALU.bypass, k01, ["lnm"])
            tt(VAR, MEAN, MEAN, ALU.mult, ["lnm"], ["lnv"])
            stt(VAR, S2, 1.0 / 64, VAR, ALU.mult, ALU.subtract, k01 + ["lnv"], ["lnv"])
            ts(VAR, VAR, 0.0, None, ALU.max, ALU.bypass, ["lnv"], ["lnv"])
            act(RSTD, VAR, AF.Ln, ["lnv"], ["lnr"], bias=EPS)
            act(RSTD, RSTD, AF.Exp, ["lnr"], ["lnr"], scale=-0.5)
            VH = abuf[:, U_VH * TT:(U_VH + 2) * TT]
            for j in range(4):
                VG3 = U(U_VG + j).rearrange("p (h d) -> p h d", h=8)
                kv = ukeys(U_VG + j)
                mb = MEAN[:, j * 8:(j + 1) * 8].rearrange("p (h o) -> p h o", o=1).to_broadcast([128, 8, 64])
                rb = RSTD[:, j * 8:(j + 1) * 8].rearrange("p (h o) -> p h o", o=1).to_broadcast([128, 8, 64])
                tt(VG3, VG3, mb, ALU.subtract, kv + ["lnm"], kv)
                tt(VH[:, j * 512:(j + 1) * 512].rearrange("p (h d) -> p h d", h=8), VG3, rb, ALU.mult, kv + ["lnr"],
                   [("a", U_VH + j // 2, j % 2)])
            for cc in range(4):
                bz = next_bank()
                for hh in range(2):
                    hd = 2 * cc + hh
                    for j in range(4):
                        c0 = j * 512 + cc * 128 + hh * 64
                        mm(psb[bz][hh * 64:(hh + 1) * 64, j * 128:(j + 1) * 128], VH[:, c0:c0 + 64],
                           wsm[:, l * 1024 + hd * 128: l * 1024 + (hd + 1) * 128], True, True,
                           [("a", U_VH + j // 2, j % 2), ("wsm", l)], [("ps", bz)])
                UUc = U(U_UU + cc)
                kuu = ukeys(U_UU + cc)
                zin = psb[bz][:].rearrange("p (j t) -> p j t", j=4)
                sbv = sbbt[:, l * 512 + cc * 128: l * 512 + (cc + 1) * 128].rearrange("p (o t) -> p o t", o=1).to_broadcast([128, 4, 128])
                g = nxt("sg")
                stt(SG(g).rearrange("p (j t) -> p j t", j=4), zin, SP(l, C_VN + cc), sbv, ALU.mult, ALU.add,
                    [("ps", bz), CONST, ("sbb", l)], [("sg", g)])
                tt(UUc, SG(g), UUc, ALU.mult, [("sg", g)] + kuu, kuu)
                q = nxt("sq")
                act(SQ(q), UUc, AF.Square, kuu, [("sq", q)])
                mm(psb[6][:], ones[:], SQ(q), cc == 0, cc == 3, [("sq", q), "ones"], [("ps", 6)])
            r = rstd_from(psb[6][:], [("ps", 6)], 512)
            for cc in range(4):
                stt(ybuf[:, (4 + cc) * ST:(5 + cc) * ST], U(U_UU + cc), SP(l, C_GN + cc), RS(r), ALU.mult, ALU.mult,
                    ukeys(U_UU + cc) + [("rs", r), CONST], [("yg", cc)])

        def outproj(l, s, W):
            par = s % 2
            for o in range(8):
                b = next_bank()
                sl, wsl = W["o0"] if o < 4 else W["o1"]
                for kq in range(8):
                    oo = ((o % 4) * 8 + kq) * 128
                    if kq < 4:
                        rhs, rk = YL(par, kq), ("yl", par, kq)
                    else:
                        rhs, rk = ybuf[:, kq * ST:(kq + 1) * ST], ("yg", kq - 4)
                    mm(psb[b][:], wsl[:, oo:oo + 128], rhs, kq == 0, kq == 7, [("ring", sl), rk], [("ps", b)])
                stt(X(o, s), psb[b][:], MOD(l, 5, o), X(o, s), ALU.mult, ALU.add,
                    [("ps", b), xk(o, s), ("mod", l, 1)], [xk(o, s)])

        def final_store(t):
            P.region = ("final", t)
            outv = outT.rearrange("(c p) t -> p c t", p=128)
            for s in range(NS):
                u0 = 14 if s == 0 else 6
                if do_final:
                    nb = next_bank()
                    sumsq([X(c, s) for c in range(8)], [[xk(c, s)] for c in range(8)], nb)
                    r = rstd_from(psb[nb][:], [("ps", nb)], D)
                for c in range(8):
                    if do_final:
                        stt(U(u0 + c), X(c, s), SP(0, C_FN + c), RS(r), ALU.mult, ALU.mult,
                            [xk(c, s), ("rs", r), CONST], ukeys(u0 + c))
                    else:
                        P.add("dve", lambda e, c=c, s=s, u0=u0: e.tensor_copy(out=U(u0 + c), in_=X(c, s)), [xk(c, s)], ukeys(u0 + c))
                t0 = t * TT + s * ST
                for hf in range(2):
                    src = abuf[:, (u0 + 4 * hf) * TT:(u0 + 4 * hf + 4) * TT].bitcast(F32).rearrange("p (c t) -> p c t", c=4)
                    dma("sp", outv[:, 4 * hf:4 * hf + 4, t0:t0 + ST], src,
                        [k_ for c in range(4 * hf, 4 * hf + 4) for k_ in ukeys(u0 + c)], [], "out")

        for t in range(NT):
            cur["t"] = t
            load_x(t, 1)
            if t + 1 < NT:
                load_x(t + 1, 0)
            for l in range(L):
                ffn(l, 0, interleave_ada=(t == 0))
                mixer(l)
                ffn(l, 1, interleave_ada=(t == 0))
            final_store(t)

        mk0 = None
        if SCHEDULE:
            mk0 = P.schedule()
        stats = P.emit(nc, st, final_waits=[("out", P.dma_counts["out"])])
        stats["sim_us"] = mk0
        build_program.stats = stats
        build_program.prog = P
    return nc


_CACHE = {}


def kernel(**inputs):
    x = np.asarray(inputs["x"])
    B, S, _ = x.shape
    L = np.asarray(inputs["w_ada"]).shape[0]
    in_maps = prep_inputs(inputs, L)
    key = (S, L)
    if key not in _CACHE:
        _CACHE[key] = build_program(S, L)
    nc = _CACHE[key]
    res = run_bass_kernel_spmd(nc, in_maps, core_ids=list(range(B)))
    out = np.stack([np.ascontiguousarray(res.results[b]["outT"].T) for b in range(B)], axis=0)
    return out.astype(np.float32)
```

```python
import numpy as np
from contextlib import ExitStack
import concourse.bass as bass
import concourse.mybir as mybir
from concourse.bass_utils import run_bass_kernel_spmd

F32 = mybir.dt.float32
F32R = mybir.dt.float32r
BF16 = mybir.dt.bfloat16
AF = mybir.ActivationFunctionType
ALU = mybir.AluOpType

D = 1024
DFF = 2816
NF = DFF // 128
NSPC = 144
EPS = 1e-6
TT = 1024
ST = 512
NS = TT // ST
NSLOT = 6
SCHEDULE = True
SLOT = 4096

ENGS = ("pe", "act", "dve", "pool", "sp")


class Op:
    __slots__ = ("eng", "fn", "deps", "signal", "sem", "val", "dma_sem", "dma_need",
                 "idx", "dur", "aset", "sdeps", "succs", "prio", "npred", "fin", "nbytes", "region", "start")

    def __init__(self, eng, fn, dma_sem):
        self.eng = eng
        self.fn = fn
        self.deps = set()
        self.dma_need = {}
        self.signal = False
        self.sem = None
        self.val = 0
        self.dma_sem = dma_sem
        self.dur = 0.2
        self.aset = None
        self.nbytes = 0


class Prog:
    def __init__(self, same_engine_sync=("act", "dve", "pool")):
        self.ops = {e: [] for e in ENGS}
        self.last_writer = {}
        self.readers = {}
        self.same_engine_sync = set(same_engine_sync)
        self.dma_counts = {}
        self.all = []
        self.region = None

    def add(self, eng, fn, reads=(), writes=(), dma_sem=None, dur=0.2, aset=None, nbytes=0):
        op = Op(eng, fn, dma_sem)
        op.idx = len(self.all)
        op.region = self.region
        op.dur = dur
        op.aset = aset
        op.nbytes = nbytes
        self.all.append(op)
        deps = op.deps
        for k in reads:
            w = self.last_writer.get(k)
            if w is not None:
                deps.add(w)
        for k in writes:
            w = self.last_writer.get(k)
            if w is not None:
                deps.add(w)
            r = self.readers.get(k)
            if r:
                deps.update(r)
        op.sdeps = list(deps)
        for d in list(deps):
            if d.dma_sem is not None:
                deps.discard(d)
                op.dma_need[d.dma_sem] = self.dma_counts[d.dma_sem]
            elif d.eng == eng and eng not in self.same_engine_sync:
                deps.discard(d)
        for k in reads:
            self.readers.setdefault(k, []).append(op)
        for k in writes:
            self.last_writer[k] = op
            self.readers[k] = []
        if dma_sem is not None:
            c = self.dma_counts.get(dma_sem, 0) + 16
            self.dma_counts[dma_sem] = c
            op.val = c
        self.ops[eng].append(op)
        return op

    def schedule(self, reorder=("pe", "act", "dve"), lat=0.12, topk=12, swpen=1.3):
        import heapq
        ops = self.all
        for op in ops:
            op.succs = []
        for op in ops:
            op.npred = len(op.sdeps)
            for d in op.sdeps:
                d.succs.append(op)
        for op in reversed(ops):
            m = 0.0
            for su in op.succs:
                if su.prio > m:
                    m = su.prio
            op.prio = op.dur + m
        ready = {e: [] for e in ENGS}
        inorder = {e: (e not in reorder) for e in ENGS}
        nextidx = {e: 0 for e in ENGS}
        orig = {e: list(self.ops[e]) for e in ENGS}
        rdy = {}
        for op in ops:
            if op.npred == 0:
                rdy[op] = 0.0
                heapq.heappush(ready[op.eng], ((op.idx if inorder[op.eng] else -op.prio), op.idx, op))
        free = {e: 0.0 for e in ENGS}
        dma_free = 0.0
        cur_set = None
        new = {e: [] for e in ENGS}
        done = 0
        n = len(ops)
        sched = set()
        while done < n:
            best = None
            popped = {}
            for e in ENGS:
                h = ready[e]
                if inorder[e]:
                    if nextidx[e] >= len(orig[e]):
                        continue
                    tgt = orig[e][nextidx[e]]
                    if tgt.npred > 0 or tgt not in rdy:
                        continue
                    cands = [tgt]
                else:
                    cands = []
                    pl = []
                    while h and len(cands) < topk:
                        x = heapq.heappop(h)
                        if x[2] in sched:
                            continue
                        pl.append(x)
                        cands.append(x[2])
                    popped[e] = pl
                for op in cands:
                    est = max(free[e], rdy[op])
                    kest = est
                    if e == "act" and op.aset is not None and op.aset != cur_set:
                        est += 1.3
                        kest += swpen
                    key = (kest, -op.prio, est)
                    if best is None or key < best[0]:
                        best = (key, op)
            assert best is not None, "scheduler deadlock"
            (_, _, est), op = best
            e = op.eng
            sched.add(op)
            for e2, pl in popped.items():
                for x in pl:
                    if x[2] is not op:
                        heapq.heappush(ready[e2], x)
            if inorder[e]:
                nextidx[e] += 1
            if e == "act" and op.aset is not None:
                cur_set = op.aset
            op.start = est
            if op.dma_sem is not None:
                free[e] = est + 0.06
                st_ = max(est, dma_free)
                dma_free = st_ + op.nbytes / 330e3
                op.fin = dma_free + 2.0
            else:
                free[e] = est + op.dur
                op.fin = free[e]
            new[e].append(op)
            done += 1
            for su in op.succs:
                su.npred -= 1
                t = op.fin + (lat if su.eng != e else 0.0)
                if rdy.get(su, 0.0) < t:
                    rdy[su] = t
                if su.npred == 0:
                    heapq.heappush(ready[su.eng], ((su.idx if inorder[su.eng] else -su.prio), su.idx, su))
        self.ops = new
        return max(op.fin for op in ops)

    def emit(self, nc, stack, final_waits=()):
        for e in ENGS:
            for op in self.ops[e]:
                for d in op.deps:
                    d.signal = True
        sems = {}
        for e in ENGS:
            sems[("eng", e)] = stack.enter_context(nc.semaphore("s_" + e))
        for k in self.dma_counts:
            sems[("dma", k)] = stack.enter_context(nc.semaphore("d_" + str(k)))
        for e in ENGS:
            c = 0
            for op in self.ops[e]:
                if op.dma_sem is not None:
                    op.sem = ("dma", op.dma_sem)
                    continue
                op.sem = ("eng", e)
                if op.signal:
                    c += 1
                    op.val = c
        block = stack.enter_context(nc.Block())
        engmap = {"pe": "tensor", "act": "scalar", "dve": "vector", "pool": "gpsimd", "sp": "sync"}
        stats = {}
        for e in ENGS:
            ops = self.ops[e]
            if not ops:
                continue

            def body(eng, ops=ops, e=e):
                waited = {}
                nw = 0
                for op in ops:
                    need = {}
                    for d in op.deps:
                        if d.val > need.get(d.sem, 0):
                            need[d.sem] = d.val
                    for k, v in op.dma_need.items():
                        need[("dma", k)] = v
                    for s, v in need.items():
                        if waited.get(s, 0) < v:
                            eng.wait_ge(sems[s], v)
                            waited[s] = v
                            nw += 1
                    ins = op.fn(eng)
                    if op.dma_sem is not None:
                        ins.then_inc(sems[op.sem], 16)
                    elif op.signal:
                        ins.then_inc(sems[op.sem], 1)
                if e == "sp":
                    for (k, v) in final_waits:
                        eng.wait_ge(sems[("dma", k)], v)
                stats[e] = (len(ops), nw)

            getattr(block, engmap[e])(body)
        return stats


C_BADA = 0
C_N1 = 72
C_NM = 80
C_N2 = 88
C_CW = 96
C_CB = 112
C_BA = 116
C_BX = 120
C_LAM = 124
C_LN = 128
C_GN = 132
C_FN = 136
C_VN = 144
NSPC = 148


def _cols(v, n):
    return np.ascontiguousarray(v.reshape(n, 128).T)


def prep_inputs(inp, L):
    f = lambda a: np.asarray(a, dtype=np.float32)
    B = inp["x"].shape[0]
    shared = {}
    shared["wada"] = np.ascontiguousarray(
        f(inp["w_ada"])[:L].reshape(L, 8, 128, 18, 512).transpose(0, 3, 2, 1, 4)).reshape(L, 18, 128, SLOT)
    sp = np.zeros((L, 128, NSPC), np.float32)
    for l in range(L):
        sp[l, :, C_BADA:C_BADA + 72] = _cols(f(inp["b_ada"])[l], 72)
        sp[l, :, C_N1:C_N1 + 8] = _cols(f(inp["ffn1_norm"])[l], 8)
        sp[l, :, C_NM:C_NM + 8] = _cols(f(inp["mix_norm"])[l], 8)
        sp[l, :, C_N2:C_N2 + 8] = _cols(f(inp["ffn2_norm"])[l], 8)
        for k in range(4):
            sp[l, :, C_CW + 4 * k:C_CW + 4 * k + 4] = _cols(f(inp["conv_w"])[l, k], 4)
        sp[l, :, C_CB:C_CB + 4] = _cols(f(inp["conv_b"])[l], 4)
        sp[l, :, C_BA:C_BA + 4] = _cols(f(inp["gate_a_b"])[l].reshape(-1), 4)
        sp[l, :, C_BX:C_BX + 4] = _cols(f(inp["gate_x_b"])[l].reshape(-1), 4)
        sp[l, :, C_LAM:C_LAM + 4] = _cols(f(inp["lru_lambda"])[l], 4)
        sp[l, :, C_LN:C_LN + 4] = _cols(f(inp["lru_out_norm"])[l], 4)
        sp[l, :, C_GN:C_GN + 4] = _cols(f(inp["gmlp_out_norm"])[l], 4)
        sp[l, :, C_FN:C_FN + 8] = _cols(f(inp["final_norm"]), 8)
        sp[l, :, C_VN:C_VN + 4] = _cols(f(inp["v_norm"])[l], 4)
    shared["sp"] = sp
    gates = np.zeros((L, 128, 2, 4, 128), np.float32)
    for l in range(L):
        for gi, nm in enumerate(("gate_a_w", "gate_x_w")):
            w = f(inp[nm])[l]
            for cc in range(4):
                gates[l, 0:64, gi, cc, 0:64] = w[2 * cc]
                gates[l, 64:128, gi, cc, 64:128] = w[2 * cc + 1]
    shared["gates"] = gates
    shared["wst"] = np.ascontiguousarray(f(inp["spatial_w"])[:L].transpose(0, 3, 1, 2))
    shared["maskt"] = np.triu(np.ones((128, 128), np.float32))
    shared["ident"] = np.eye(128, dtype=np.float32)
    sbb = f(inp["spatial_b"])[:L]
    shared["sbb"] = np.ascontiguousarray(np.repeat(sbb, 64, axis=1).reshape(L, 4, 128, 128).transpose(0, 2, 1, 3))
    wgu = np.stack([f(inp["ffn1_w_gu"])[:L], f(inp["ffn2_w_gu"])[:L]], axis=1)
    shared["wgu"] = np.ascontiguousarray(
        wgu.reshape(L, 2, 8, 128, 2, 11, 2, 128).transpose(0, 1, 5, 3, 6, 2, 4, 7)).reshape(L, 2, 11, 128, SLOT)
    wd = np.stack([f(inp["ffn1_w_down"])[:L], f(inp["ffn2_w_down"])[:L]], axis=1)
    shared["wd"] = np.ascontiguousarray(
        wd.reshape(L, 2, NF, 128, 8, 128).transpose(0, 1, 4, 3, 2, 5)).reshape(L, 2, 8, 128, NF * 128)
    win = f(inp["w_in"])[:L]
    p012 = win[:, :, :1536].reshape(L, 8, 128, 3, 4, 128).transpose(0, 3, 2, 4, 1, 5).reshape(L, 3, 128, SLOT)
    p3 = win[:, :, 1536:].reshape(L, 8, 128, 512).transpose(0, 2, 1, 3).reshape(L, 1, 128, SLOT)
    shared["win"] = np.ascontiguousarray(np.concatenate([p012, p3], axis=1))
    wout = f(inp["w_out"])[:L]
    shared["wout"] = np.ascontiguousarray(
        wout.reshape(L, 8, 128, 2, 4, 128).transpose(0, 3, 2, 4, 1, 5)).reshape(L, 2, 128, SLOT)
    per_core = []
    x = f(inp["x"])
    c = f(inp["c"])
    for b in range(B):
        d = dict(shared)
        d["xT"] = np.ascontiguousarray(x[b].T)
        d["cv"] = _cols(c[b], 8)
        per_core.append(d)
    return per_core


def build_program(S, L, do_final=True):
    NT = S // TT
    nc = bass.Bass("TRN2", target_bir_lowering=False)
    dr = lambda name, shape: nc.dram_tensor(name, shape, F32, kind="ExternalInput").ap()
    xT = dr("xT", [D, S])
    cv = dr("cv", [128, 8])
    wada = dr("wada", [L, 18, 128, SLOT])
    spd = dr("sp", [L, 128, NSPC])
    gates_d = dr("gates", [L, 128, 2, 4, 128])
    wst_d = dr("wst", [L, 128, 8, 128])
    maskt_d = dr("maskt", [128, 128])
    ident_d = dr("ident", [128, 128])
    sbb_d = dr("sbb", [L, 128, 4, 128])
    wgu_d = dr("wgu", [L, 2, 11, 128, SLOT])
    wd_d = dr("wd", [L, 2, 8, 128, NF * 128])
    win_d = dr("win", [L, 4, 128, SLOT])
    wout_d = dr("wout", [L, 2, 128, SLOT])
    outT = nc.dram_tensor("outT", [D, S], F32, kind="ExternalOutput").ap()
    AX = mybir.AxisListType.X

    P = Prog()
    with ExitStack() as st:
        sb = lambda name, shape, dt: st.enter_context(nc.sbuf_tensor("sb_" + name, shape, dt))
        xh = sb("xh", [128, 3 * 8 * ST], F32)
        hbuf = sb("hbuf", [128, 8 * TT], BF16)
        abuf = sb("abuf", [128, NF * TT], BF16)
        ring = sb("ring", [128, NSLOT * SLOT], BF16)
        ybuf = sb("ybuf", [128, 8 * ST], BF16)
        xlp = sb("xlp", [128, 2 * (ST + 4)], F32)
        sqb = sb("sqb", [128, 4 * ST], BF16)
        sgb = sb("sgb", [128, 3 * ST], F32)
        rsb = sb("rsb", [128, 2 * ST], F32)
        spt = sb("spt", [128, L * NSPC], F32)
        modt = sb("modt", [128, L * 72], F32)
        dvt = sb("dvt", [128, L * 64], F32)
        gat = sb("gat", [128, L * 1024], BF16)
        wsm = sb("wsm", [128, L * 1024], BF16)
        sbbt = sb("sbbt", [128, L * 512], F32)
        ybl2 = sb("ybl2", [128, 4 * ST], BF16)
        stg = ybl2[:].bitcast(F32)
        maskt = sb("maskt", [128, 128], F32)
        ident = sb("ident", [128, 128], F32)
        ones = sb("ones", [128, 128], BF16)
        cvt = sb("cvt", [128, 8], F32)
        scb = sb("scb", [128, 8 * 128], BF16)
        hist = sb("hist", [128, L * 4 * 4], F32)
        hstate = sb("hstate", [128, L * 4], F32)
        lns = sb("lns", [128, 5 * 32], F32)
        lt = sb("lt", [128, 8], F32)
        psb = [st.enter_context(nc.psum_tensor("ps%d" % i, [128, 512], F32)) for i in range(8)]

        cur = {"t": 0}

        def xbuf(t, s):
            return (2 * t + s) % 3

        def X(c, s):
            b_ = xbuf(cur["t"], s)
            return xh[:, (b_ * 8 + c) * ST:(b_ * 8 + c + 1) * ST]

        def xk(c, s):
            return ("x", xbuf(cur["t"], s), c)

        def H(c, s):
            return hbuf[:, c * TT + s * ST: c * TT + (s + 1) * ST]

        def A(f_, s):
            return abuf[:, f_ * TT + s * ST: f_ * TT + (s + 1) * ST]

        def U(u):
            return abuf[:, u * TT:(u + 1) * TT].bitcast(F32)

        def ukeys(u):
            return [("a", u, 0), ("a", u, 1)]

        def slot_ap(sl):
            return ring[:, sl * SLOT:(sl + 1) * SLOT]

        def SQ(i):
            return sqb[:, i * ST:(i + 1) * ST]

        def SG(i):
            return sgb[:, i * ST:(i + 1) * ST]

        def RS(i):
            return rsb[:, i * ST:(i + 1) * ST]

        def SP(l, col, n=1):
            return spt[:, l * NSPC + col: l * NSPC + col + n]

        def MOD(l, m, c):
            return modt[:, l * 72 + m * 8 + c: l * 72 + m * 8 + c + 1]

        DV_GM1, DV_HG1, DV_GMM, DV_GM2, DV_HG2, DV_C8H, DV_C16H, DV_HB = 0, 8, 16, 24, 32, 40, 44, 48

        def DV(l, base, c, n=1):
            return dvt[:, l * 64 + base + c: l * 64 + base + c + n]

        cnt = {"bank": 0, "slot": 0, "sq": 0, "sg": 0, "rs": 0}

        POOLS = {
            "bank": {"all": [0, 1, 2, 3, 4, 5, 6, 7], "L": [0, 1, 2], "G": [3, 4, 5]},
            "sq": {"all": [0, 1, 2, 3], "L": [0, 1], "G": [2, 3]},
            "sg": {"all": [0, 1, 2], "L": [0, 1], "G": [2]},
            "rs": {"all": [0, 1], "L": [0], "G": [1]},
            "slot": {"all": list(range(NSLOT))},
        }
        pool = {"cur": "all"}
        pcnt = {}

        def nxt(name, n=None):
            p = pool["cur"] if pool["cur"] in POOLS[name] else "all"
            lst = POOLS[name][p]
            i = pcnt.get((name, p), 0)
            pcnt[(name, p)] = (i + 1) % len(lst)
            return lst[i]

        def next_bank():
            return nxt("bank")

        def fsz(ap):
            n = 1
            for d in ap.shape[1:]:
                n *= d
            return n

        ASET = {AF.Silu: "silu", AF.Gelu_apprx_tanh: "gelu", AF.Tanh: "gelu", AF.Exp: "lnexp", AF.Ln: "lnexp"}

        def mm(out, lhsT, rhs, start, stop, reads, writes):
            P.add("pe", lambda e: e.matmul(out, lhsT=lhsT, rhs=rhs, start=start, stop=stop), reads, writes,
                  dur=0.004 + fsz(rhs) / 2400.0)

        def act(out, in_, func, reads, writes, **kw):
            P.add("act", lambda e: e.activation(out=out, in_=in_, func=func, **kw), reads, writes,
                  dur=0.2 + fsz(out) / 1300.0, aset=ASET.get(func))

        def tt(out, in0, in1, op, reads, writes):
            P.add("dve", lambda e: e.tensor_tensor(out=out, in0=in0, in1=in1, op=op), reads, writes, dur=0.1 + fsz(out) / 870.0)

        def ts(out, in0, s1, s2, op0, op1, reads, writes):
            P.add("dve", lambda e: e.tensor_scalar(out=out, in0=in0, scalar1=s1, scalar2=s2, op0=op0, op1=op1), reads, writes,
                  dur=0.1 + fsz(out) / 900.0)

        def stt(out, in0, scalar, in1, op0, op1, reads, writes):
            P.add("dve", lambda e: e.scalar_tensor_tensor(out=out, in0=in0, scalar=scalar, in1=in1, op0=op0, op1=op1), reads, writes,
                  dur=0.12 + fsz(out) / 820.0)

        def red(out, in_, reads, writes):
            P.add("dve", lambda e: e.tensor_reduce(out=out, in_=in_, axis=AX, op=ALU.add), reads, writes, dur=0.1 + fsz(in_) / 1000.0)

        def dma(eng, out, in_, reads, writes, sem):
            nb = 4
            for d in in_.shape:
                nb *= d
            P.add(eng, lambda e: e.dma_start(out=out, in_=in_), reads, writes, dma_sem=sem, nbytes=nb)

        def ring_load(src, extra_reads=()):
            sl = nxt("slot", NSLOT)
            n = src.shape[-1]
            dst = slot_ap(sl)
            dma("pool", dst[:, 0:n].rearrange("p (a b) -> p a b", a=2), src.rearrange("p (a b) -> p a b", a=2),
                list(extra_reads), [("ring", sl)], "ring%d" % sl)
            return sl, dst

        def rstd_from(ps_ap, pkeys, n_feat):
            r = nxt("rs")
            act(RS(r), ps_ap, AF.Ln, pkeys, [("rs", r)], scale=1.0 / n_feat, bias=EPS)
            act(RS(r), RS(r), AF.Exp, [("rs", r)], [("rs", r)], scale=-0.5)
            return r

        def load_x(t, s):
            b_ = xbuf(t, s)
            t0 = t * TT + s * ST
            dma("sp", xh[:, b_ * 8 * ST:(b_ + 1) * 8 * ST].rearrange("p (c t) -> p c t", c=8),
                xT.rearrange("(c p) t -> p c t", p=128)[:, :, t0:t0 + ST], [], [("x", b_, c) for c in range(8)], "xin%d" % b_)

        CONST = "const"
        for l in range(L):
            dma("sp", spt[:, l * NSPC:(l + 1) * NSPC], spd[l], [], [CONST], "c0")
        dma("sp", cvt[:], cv, [], [CONST], "c0")
        dma("sp", ident[:], ident_d, [], [CONST], "c0")
        load_x_early = True
        P.add("dve", lambda e: e.memset(ones[:], 1.0), [], ["ones"])
        P.add("dve", lambda e: e.memset(hist[:], 0.0), [], [("hist", l, cc) for l in range(L) for cc in range(4)])
        P.add("dve", lambda e: e.memset(hstate[:], 0.0), [], [("hstate", l, cc) for l in range(L) for cc in range(4)])
        for l in range(L):
            ts(DV(l, DV_HB, 0, 4), SP(l, C_BA, 4), 0.5, None, ALU.mult, ALU.bypass, [CONST], [("dvb", l)])
            ts(DV(l, DV_HB, 4, 4), SP(l, C_BX, 4), 0.5, None, ALU.mult, ALU.bypass, [CONST], [("dvb", l)])
        act(scb[:].rearrange("p (k m) -> p k m", k=8),
            cvt[:].rearrange("p (k o) -> p k o", o=1).to_broadcast([128, 8, 128]), AF.Silu, [CONST], ["scb"])

        def ada_piece(l, n, extra_reads=()):
            sl, w = ring_load(wada[l, n], extra_reads)
            b = next_bank()
            for kc in range(8):
                mm(psb[b][:], scb[:, kc * 128:(kc + 1) * 128], w[:, kc * 512:(kc + 1) * 512], kc == 0, kc == 7,
                   [("ring", sl), "scb"], [("ps", b)])
            g = nxt("sg")
            tt(SG(g).rearrange("p (q j) -> p q j", q=4), psb[b][:].rearrange("p (q j) -> p q j", q=4),
               ident[:].rearrange("p (o j) -> p o j", o=1).to_broadcast([128, 4, 128]), ALU.mult,
               [("ps", b), CONST], [("sg", g)])
            red(modt[:, l * 72 + 4 * n: l * 72 + 4 * n + 4], SG(g).rearrange("p (q j) -> p q j", q=4),
                [("sg", g)], [("modraw", l, n)])

        def ada_finish(l, g):
            c0 = l * 72 + g * 24
            pk = [("modraw", l, n) for n in range(6 * g, 6 * g + 6)]
            tt(modt[:, c0:c0 + 24], modt[:, c0:c0 + 24], SP(l, C_BADA + g * 24, 24), ALU.add, pk + [CONST], [("mod", l, g)] + pk)
            rk = [("mod", l, g), CONST]
            wk = [("dv", l, g)]
            ncol = (C_N1, C_NM, C_N2)[g]
            gmb = (DV_GM1, DV_GMM, DV_GM2)[g]
            stt(DV(l, gmb, 0, 8), modt[:, c0 + 8:c0 + 16], 1.0, SP(l, ncol, 8), ALU.add, ALU.mult, rk, wk)
            if g != 1:
                hgb = DV_HG1 if g == 0 else DV_HG2
                ts(DV(l, hgb, 0, 8), modt[:, c0 + 16:c0 + 24], 0.5, None, ALU.mult, ALU.bypass, rk, wk)

        def lam_consts(l):
            act(lt[:, 0:4], SP(l, C_LAM, 4), AF.Exp, [CONST], ["lt"], scale=-1.0)
            act(lt[:, 4:8], lt[:, 0:4], AF.Ln, ["lt"], ["lt"], bias=1.0)
            ts(DV(l, DV_C8H, 0, 4), lt[:, 4:8], -4.0, None, ALU.mult, ALU.bypass, ["lt"], [("dvl", l)])
            ts(DV(l, DV_C16H, 0, 4), lt[:, 4:8], -8.0, None, ALU.mult, ALU.bypass, ["lt"], [("dvl", l)])

        for l in range(L):
            lam_consts(l)
        load_x(0, 0)
        for n in range(6):
            ada_piece(0, n, extra_reads=["scb"] + [("x", xbuf(0, 0), c) for c in range(8)] if n == 0 else ())
        ada_finish(0, 0)
        dma("sp", maskt[:], maskt_d, [], ["maskt"], "c2")
        for l in range(L):
            dma("sp", sbbt[:, l * 512:(l + 1) * 512], sbb_d[l].rearrange("p c t -> p (c t)"), [], [("sbb", l)], "c2")
        stgk = ["stg"] + [("yl", 1, cc) for cc in range(4)]
        for l in range(L):
            dma("sp", stg, gates_d[l].rearrange("p g c j -> p (g c j)"), [], stgk, "c1")
            P.add("dve", lambda e, l=l: e.tensor_copy(out=gat[:, l * 1024:(l + 1) * 1024], in_=stg), stgk, [("gat", l)])
            dma("sp", stg, wst_d[l].rearrange("p h t -> p (h t)"), [], stgk, "c1")
            tt(wsm[:, l * 1024:(l + 1) * 1024].rearrange("p (h t) -> p h t", h=8),
               stg.rearrange("p (h t) -> p h t", h=8),
               maskt[:].rearrange("p (o t) -> p o t", o=1).to_broadcast([128, 8, 128]), ALU.mult, stgk + ["maskt"], [("wsm", l)])
        pending_ada = [(0, n) for n in range(6, 18)] + [(l, n) for l in range(1, L) for n in range(18)]

        def ada_some(k=1):
            for _ in range(k):
                if pending_ada:
                    l_, n_ = pending_ada.pop(0)
                    ada_piece(l_, n_)
                    if n_ % 6 == 5:
                        ada_finish(l_, n_ // 6)

        def ada_need(l, g):
            while pending_ada and (pending_ada[0][0], pending_ada[0][1] // 6) <= (l, g):
                ada_some(1)

        def sumsq(srcs, skeys, bank):
            n = len(srcs)
            for i, (src, sk) in enumerate(zip(srcs, skeys)):
                q = nxt("sq")
                act(SQ(q), src, AF.Square, sk, [("sq", q)])
                mm(psb[bank][:], ones[:], SQ(q), i == 0, i == n - 1, [("sq", q), "ones"], [("ps", bank)])

        def norm_to_h(l, s, gmbase, shift_m):
            nb = next_bank()
            sumsq([X(c, s) for c in range(8)], [[xk(c, s)] for c in range(8)], nb)
            r = rstd_from(psb[nb][:], [("ps", nb)], D)
            for c in range(8):
                g = nxt("sg")
                tt(SG(g), X(c, s), RS(r), ALU.mult, [xk(c, s), ("rs", r)], [("sg", g)])
                gq = shift_m // 3
                act(H(c, s), SG(g), AF.Identity, [("sg", g), ("dv", l, gq), ("mod", l, gq)], [("h", c, s)],
                    scale=DV(l, gmbase, c), bias=MOD(l, shift_m, c))

        def ffn(l, w, interleave_ada=False):
            P.region = ("ffn", cur["t"], l, w)
            ada_need(l, 0 if w == 0 else 2)
            gmbase = DV_GM1 if w == 0 else DV_GM2
            hgbase = DV_HG1 if w == 0 else DV_HG2
            shift_m = 0 if w == 0 else 6
            for s in range(NS):
                norm_to_h(l, s, gmbase, shift_m)
            for piece in range(11):
                sl, wsl = ring_load(wgu_d[l, w, piece])
                for fi in range(2):
                    f_ = 2 * piece + fi
                    for s in range(NS):
                        bg = next_bank()
                        bu = next_bank()
                        for gu, bb in ((0, bg), (1, bu)):
                            for kc in range(8):
                                o = ((fi * 8 + kc) * 2 + gu) * 128
                                mm(psb[bb][:], wsl[:, o:o + 128], H(kc, s), kc == 0, kc == 7,
                                   [("ring", sl), ("h", kc, s)], [("ps", bb)])
                        g = nxt("sg")
                        act(SG(g), psb[bg][:], AF.Silu, [("ps", bg)], [("sg", g)])
                        tt(A(f_, s), psb[bu][:], SG(g), ALU.mult, [("ps", bu), ("sg", g)], [("a", f_, s)])
                if interleave_ada and piece % 2 == 1:
                    ada_some(1)
            for o in range(8):
                sl, wsl = ring_load(wd_d[l, w, o])
                for s in range(NS):
                    b = next_bank()
                    for f_ in range(NF):
                        mm(psb[b][:], wsl[:, f_ * 128:(f_ + 1) * 128], A(f_, s), f_ == 0, f_ == NF - 1,
                           [("ring", sl), ("a", f_, s)], [("ps", b)])
                    stt(X(o, s), psb[b][:], DV(l, hgbase, o), X(o, s), ALU.mult, ALU.add,
                        [("ps", b), xk(o, s), ("dv", l, 0 if w == 0 else 2)], [xk(o, s)])
                if interleave_ada and o % 2 == 1:
                    ada_some(1)

        U_UU, U_VG, U_SQV, U_VH = 11, 15, 19, 20

        def mixer(l):
            P.region = ("mixer", cur["t"], l)
            ada_need(l, 1)
            W = {}
            for nm, src in (("xl", win_d[l, 0]), ("gl", win_d[l, 1]), ("u", win_d[l, 2]), ("v", win_d[l, 3]),
                            ("o0", wout_d[l, 0]), ("o1", wout_d[l, 1])):
                W[nm] = ring_load(src)
            for s in range(NS):
                pool["cur"] = "L"
                norm_to_h(l, s, DV_GMM, 3)
                lru(l, s, W)
            if cur["t"] == 0:
                ada_some(2)
            for s in range(NS):
                pool["cur"] = "G"
                gmlp(l, s, W)
                if s == NS - 1 and cur["t"] == 0:
                    ada_some(2)
                outproj(l, s, W)
            pool["cur"] = "all"

        def YL(par, cc):
            base = ybuf if par == 0 else ybl2
            return base[:, cc * ST:(cc + 1) * ST]

        def lru(l, s, W):
            sl_xl, w_xl = W["xl"]
            sl_gl, w_gl = W["gl"]
            dk = [("dvl", l), CONST]
            par = s % 2
            for cc in range(4):
                k = cc % 2
                XC, R, I_, MU, GG = U(0 + k), U(2 + k), U(4 + k), U(6 + k), U(8 + k)
                kx, kr, ki, km, kg = ukeys(0 + k), ukeys(2 + k), ukeys(4 + k), ukeys(6 + k), ukeys(8 + k)
                XB = abuf[:, 10 * TT + k * ST: 10 * TT + (k + 1) * ST]
                kxb = [("a", 10, k)]
                b = next_bank()
                for kc in range(8):
                    o = (cc * 8 + kc) * 128
                    mm(psb[b][:], w_xl[:, o:o + 128], H(kc, s), kc == 0, kc == 7, [("ring", sl_xl), ("h", kc, s)], [("ps", b)])
                xl_ = xlp[:, k * (ST + 4): (k + 1) * (ST + 4)]
                kxl = [("xlp", k)]
                hv = hist[:, (l * 4 + cc) * 4:(l * 4 + cc) * 4 + 3]
                act(xl_[:, 0:3], hv, AF.Copy, [("hist", l, cc)], kxl)
                act(xl_[:, 3:3 + ST], psb[b][:], AF.Copy, [("ps", b)], kxl)
                act(XC, xl_[:, 0:ST], AF.Identity, kxl + [CONST], kx, scale=SP(l, C_CW + cc), bias=SP(l, C_CB + cc))
                for kk in range(1, 4):
                    stt(XC, xl_[:, kk:kk + ST], SP(l, C_CW + 4 * kk + cc), XC, ALU.mult, ALU.add, kxl + [CONST] + kx, kx)
                act(hv, xl_[:, ST:ST + 3], AF.Copy, kxl, [("hist", l, cc)])
                act(XB, XC, AF.Copy, kx, kxb)
                for gi, dst, kd in ((0, R, kr), (1, I_, ki)):
                    bgt = next_bank()
                    gw = gat[:, l * 1024 + gi * 512 + cc * 128: l * 1024 + gi * 512 + (cc + 1) * 128]
                    mm(psb[bgt][:], gw, XB, True, True, kxb + [("gat", l)], [("ps", bgt)])
                    act(dst, psb[bgt][:], AF.Tanh, [("ps", bgt), ("dvb", l)], kd, scale=0.5, bias=DV(l, DV_HB + gi * 4, cc))
                b = next_bank()
                for kc in range(8):
                    o = (cc * 8 + kc) * 128
                    mm(psb[b][:], w_gl[:, o:o + 128], H(kc, s), kc == 0, kc == 7, [("ring", sl_gl), ("h", kc, s)], [("ps", b)])
                act(GG, psb[b][:], AF.Gelu_apprx_tanh, [("ps", b)], kg)
                act(MU, R, AF.Exp, kr + dk, km, scale=DV(l, DV_C16H, cc), bias=DV(l, DV_C16H, cc))
                act(MU, MU, AF.Ln, km, km, scale=-1.0, bias=1.0)
                act(MU, MU, AF.Exp, km, km, scale=0.5)
                act(R, R, AF.Exp, kr + dk, kr, scale=DV(l, DV_C8H, cc), bias=DV(l, DV_C8H, cc))
                stt(XC, I_, 1.0, XC, ALU.add, ALU.mult, ki + kx, kx)
                stt(MU, XC, 0.5, MU, ALU.mult, ALU.mult, kx + km, km)
                hs = hstate[:, l * 4 + cc:l * 4 + cc + 1]
                P.add("dve", lambda e, I_=I_, R=R, MU=MU, hs=hs: e.tensor_tensor_scan(
                    out=I_, data0=R, data1=MU, initial=hs, op0=ALU.mult, op1=ALU.add),
                    kr + km + [("hstate", l, cc)], ki, dur=1.35)
                act(hs, I_[:, ST - 1:ST], AF.Copy, ki, [("hstate", l, cc)])
                tt(GG, I_, GG, ALU.mult, ki + kg, kg)
                act(YL(par, cc), GG, AF.Identity, kg + [CONST], [("yl", par, cc)], scale=SP(l, C_LN + cc))
                q = nxt("sq")
                act(SQ(q), GG, AF.Square, kg, [("sq", q)])
                mm(psb[7][:], ones[:], SQ(q), cc == 0, cc == 3, [("sq", q), "ones"], [("ps", 7)])
            r = rstd_from(psb[7][:], [("ps", 7)], 512)
            for cc in range(4):
                tt(YL(par, cc), YL(par, cc), RS(r), ALU.mult, [("yl", par, cc), ("rs", r)], [("yl", par, cc)])

        def gmlp(l, s, W):
            sl_u, w_u = W["u"]
            sl_v, w_v = W["v"]
            for cc in range(4):
                b = next_bank()
                for kc in range(8):
                    o = (cc * 8 + kc) * 128
                    mm(psb[b][:], w_u[:, o:o + 128], H(kc, s), kc == 0, kc == 7, [("ring", sl_u), ("h", kc, s)], [("ps", b)])
                act(U(U_UU + cc), psb[b][:], AF.Gelu_apprx_tanh, [("ps", b)], ukeys(U_UU + cc))
            S1, S2, MEAN, VAR, RSTD = (lns[:, i * 32:(i + 1) * 32] for i in range(5))
            for j in range(4):
                b = next_bank()
                for kc in range(8):
                    mm(psb[b][:], H(kc, s)[:, j * 128:(j + 1) * 128], w_v[:, kc * 512:(kc + 1) * 512], kc == 0, kc == 7,
                       [("ring", sl_v), ("h", kc, s)], [("ps", b)])
                VG = U(U_VG + j)
                kv = ukeys(U_VG + j)
                act(VG, psb[b][:], AF.Gelu_apprx_tanh, [("ps", b)], kv)
                SQV = abuf[:, U_SQV * TT + (j % 2) * ST: U_SQV * TT + (j % 2 + 1) * ST]
                ksq = [("a", U_SQV, j % 2)]
                act(SQV, VG, AF.Square, kv, ksq)
                red(S1[:, j * 8:(j + 1) * 8], VG.rearrange("p (h d) -> p h d", h=8), kv, [("lns", 0, j)])
                red(S2[:, j * 8:(j + 1) * 8], SQV.rearrange("p (h d) -> p h d", h=8), ksq, [("lns", 1, j)])
            k01 = [("lns", 0, j) for j in range(4)] + [("lns", 1, j) for j in range(4)]
            ts(MEAN, S1, 1.0 / 64, None, ALU.mult, ALU.bypass, k01, ["lnm"])
            tt(VAR, MEAN, MEAN, ALU.mult, ["lnm"], ["lnv"])
            stt(VAR, S2, 1.0 / 64, VAR, ALU.mult, ALU.subtract, k01 + ["lnv"], ["lnv"])
            ts(VAR, VAR, 0.0, None, ALU.max, ALU.bypass, ["lnv"], ["lnv"])
            act(RSTD, VAR, AF.Ln, ["lnv"], ["lnr"], bias=EPS)
            act(RSTD, RSTD, AF.Exp, ["lnr"], ["lnr"], scale=-0.5)
            VH = abuf[:, U_VH * TT:(U_VH + 2) * TT]
            for j in range(4):
                VG3 = U(U_VG + j).rearrange("p (h d) -> p h d", h=8)
                kv = ukeys(U_VG + j)
                mb = MEAN[:, j * 8:(j + 1) * 8].rearrange("p (h o) -> p h o", o=1).to_broadcast([128, 8, 64])
                rb = RSTD[:, j * 8:(j + 1) * 8].rearrange("p (h o) -> p h o", o=1).to_broadcast([128, 8, 64])
                tt(VG3, VG3, mb, ALU.subtract, kv + ["lnm"], kv)
                tt(VH[:, j * 512:(j + 1) * 512].rearrange("p (h d) -> p h d", h=8), VG3, rb, ALU.mult, kv + ["lnr"],
                   [("a", U_VH + j // 2, j % 2)])
            for cc in range(4):
                bz = next_bank()
                for hh in range(2):
                    hd = 2 * cc + hh
                    for j in range(4):
                        c0 = j * 512 + cc * 128 + hh * 64
                        mm(psb[bz][hh * 64:(hh + 1) * 64, j * 128:(j + 1) * 128], VH[:, c0:c0 + 64],
                           wsm[:, l * 1024 + hd * 128: l * 1024 + (hd + 1) * 128], True, True,
                           [("a", U_VH + j // 2, j % 2), ("wsm", l)], [("ps", bz)])
                UUc = U(U_UU + cc)
                kuu = ukeys(U_UU + cc)
                zin = psb[bz][:].rearrange("p (j t) -> p j t", j=4)
                sbv = sbbt[:, l * 512 + cc * 128: l * 512 + (cc + 1) * 128].rearrange("p (o t) -> p o t", o=1).to_broadcast([128, 4, 128])
                g = nxt("sg")
                stt(SG(g).rearrange("p (j t) -> p j t", j=4), zin, SP(l, C_VN + cc), sbv, ALU.mult, ALU.add,
                    [("ps", bz), CONST, ("sbb", l)], [("sg", g)])
                tt(UUc, SG(g), UUc, ALU.mult, [("sg", g)] + kuu, kuu)
                q = nxt("sq")
                act(SQ(q), UUc, AF.Square, kuu, [("sq", q)])
                mm(psb[6][:], ones[:], SQ(q), cc == 0, cc == 3, [("sq", q), "ones"], [("ps", 6)])
            r = rstd_from(psb[6][:], [("ps", 6)], 512)
            for cc in range(4):
                stt(ybuf[:, (4 + cc) * ST:(5 + cc) * ST], U(U_UU + cc), SP(l, C_GN + cc), RS(r), ALU.mult, ALU.mult,
                    ukeys(U_UU + cc) + [("rs", r), CONST], [("yg", cc)])

        def outproj(l, s, W):
            par = s % 2
            for o in range(8):
                b = next_bank()
                sl, wsl = W["o0"] if o < 4 else W["o1"]
                for kq in range(8):
                    oo = ((o % 4) * 8 + kq) * 128
                    if kq < 4:
                        rhs, rk = YL(par, kq), ("yl", par, kq)
                    else:
                        rhs, rk = ybuf[:, kq * ST:(kq + 1) * ST], ("yg", kq - 4)
                    mm(psb[b][:], wsl[:, oo:oo + 128], rhs, kq == 0, kq == 7, [("ring", sl), rk], [("ps", b)])
                stt(X(o, s), psb[b][:], MOD(l, 5, o), X(o, s), ALU.mult, ALU.add,
                    [("ps", b), xk(o, s), ("mod", l, 1)], [xk(o, s)])

        def final_store(t):
            P.region = ("final", t)
            outv = outT.rearrange("(c p) t -> p c t", p=128)
            for s in range(NS):
                u0 = 14 if s == 0 else 6
                if do_final:
                    nb = next_bank()
                    sumsq([X(c, s) for c in range(8)], [[xk(c, s)] for c in range(8)], nb)
                    r = rstd_from(psb[nb][:], [("ps", nb)], D)
                for c in range(8):
                    if do_final:
                        stt(U(u0 + c), X(c, s), SP(0, C_FN + c), RS(r), ALU.mult, ALU.mult,
                            [xk(c, s), ("rs", r), CONST], ukeys(u0 + c))
                    else:
                        P.add("dve", lambda e, c=c, s=s, u0=u0: e.tensor_copy(out=U(u0 + c), in_=X(c, s)), [xk(c, s)], ukeys(u0 + c))
                t0 = t * TT + s * ST
                for hf in range(2):
                    src = abuf[:, (u0 + 4 * hf) * TT:(u0 + 4 * hf + 4) * TT].bitcast(F32).rearrange("p (c t) -> p c t", c=4)
                    dma("sp", outv[:, 4 * hf:4 * hf + 4, t0:t0 + ST], src,
                        [k_ for c in range(4 * hf, 4 * hf + 4) for k_ in ukeys(u0 + c)], [], "out")

        for t in range(NT):
            cur["t"] = t
            load_x(t, 1)
            if t + 1 < NT:
                load_x(t + 1, 0)
            for l in range(L):
                ffn(l, 0, interleave_ada=(t == 0))
                mixer(l)
                ffn(l, 1, interleave_ada=(t == 0))
            final_store(t)

        mk0 = None
        if SCHEDULE:
            mk0 = P.schedule()
        stats = P.emit(nc, st, final_waits=[("out", P.dma_counts["out"])])
        stats["sim_us"] = mk0
        build_program.stats = stats
        build_program.prog = P
    return nc


_CACHE = {}


def kernel(**inputs):
    x = np.asarray(inputs["x"])
    B, S, _ = x.shape
    L = np.asarray(inputs["w_ada"]).shape[0]
    in_maps = prep_inputs(inputs, L)
    key = (S, L)
    if key not in _CACHE:
        _CACHE[key] = build_program(S, L)
    nc = _CACHE[key]
    res = run_bass_kernel_spmd(nc, in_maps, core_ids=list(range(B)))
    out = np.stack([np.ascontiguousarray(res.results[b]["outT"].T) for b in range(B)], axis=0)
    return out.astype(np.float32)
```
